# Optimizing a Trainium2 kernel written in Bass

```python
import functools
import jax, jax.numpy as jnp
from jax import lax
import numpy as np

D_MODEL = 2048
BATCH = 4
SEQ = 2048
DEPTH = 4
DEC_BATCH = 8
DEC_SEQ = 1
PAST_LEN = 16384
PAGE_SIZE = 128

HEAD_DIM = 128
WIDTH_A = D_MODEL // 2
N_HEADS_A = WIDTH_A // HEAD_DIM
DILATIONS = ((128, 1), (512, 4), (2048, 16))
MAX_WINDOW = 2048
N_HEADS_B = 8
DV_B = (D_MODEL // 2) // N_HEADS_B
DK_B = DV_B // 2
WIDTH_BQK = N_HEADS_B * DK_B
WIDTH_BV = N_HEADS_B * DV_B
MIX_WIDTH = WIDTH_A + WIDTH_BV
IN_WIDTH = 3 * WIDTH_A + 2 * WIDTH_BQK + 2 * WIDTH_BV
FFN_DIM = 5632
PLE_DIM = 256
RET_CHUNK = 128
LN_EPS = 1e-5
GN_EPS = 1e-5
NEG_INF = -1e30
DEEPNORM_ALPHA = (2 * DEPTH) ** 0.25
DEEPNORM_BETA = (8 * DEPTH) ** -0.25

kernel_name = "hybrid_dilated_attn_retention_decoder_step"


def alibi_slopes():
    h = jnp.arange(1, N_HEADS_A + 1, dtype=jnp.float32)
    return jnp.exp2(-8.0 * h / N_HEADS_A)


def retention_log_gamma():
    h = jnp.arange(N_HEADS_B, dtype=jnp.float32)
    return jnp.log(1.0 - jnp.exp2(-5.0 - h))


def layer_norm(x, g, b):
    xf = x.astype(jnp.float32)
    mu = xf.mean(-1, keepdims=True)
    var = jnp.mean(jnp.square(xf - mu), -1, keepdims=True)
    return ((xf - mu) * lax.rsqrt(var + LN_EPS) * g + b).astype(x.dtype)


def swiglu(x, w1, w3, w2):
    return (jax.nn.silu(x @ w1) * (x @ w3)) @ w2


def split_projection(h, w_in):
    B, T, _ = h.shape
    z = h @ w_in
    bounds = np.cumsum([WIDTH_A, WIDTH_A, WIDTH_A, WIDTH_BQK, WIDTH_BQK, WIDTH_BV]).tolist()
    qa, ka, va, qb, kb, vb, gb = jnp.split(z, bounds, axis=-1)
    heads = lambda a, n: a.reshape(B, T, n, -1)
    return (heads(qa, N_HEADS_A), heads(ka, N_HEADS_A), heads(va, N_HEADS_A),
            heads(qb, N_HEADS_B), heads(kb, N_HEADS_B) * (DK_B ** -0.5), heads(vb, N_HEADS_B), gb)


def strided_band_attention(q, k, v, window, dil):
    B, T, H, Dh = q.shape
    steps = window // dil
    L = T // dil
    C = steps
    nb = -(-L // C)
    Lp = nb * C

    def to_blocks(a):
        a = a.reshape(B, L, dil, H, Dh).transpose(0, 2, 1, 3, 4)
        a = jnp.pad(a, ((0, 0), (0, 0), (0, Lp - L), (0, 0), (0, 0)))
        return a.reshape(B, dil, nb, C, H, Dh)

    def with_prev(a):
        prev = jnp.pad(a[:, :, :-1], ((0, 0), (0, 0), (1, 0), (0, 0), (0, 0), (0, 0)))
        return jnp.concatenate([prev, a], axis=3)

    qs = to_blocks(q)
    kb = with_prev(to_blocks(k))
    vb = with_prev(to_blocks(v))
    s = jnp.einsum('brnqhd,brnkhd->brnhqk', qs, kb,
                   preferred_element_type=jnp.float32) * (HEAD_DIM ** -0.5)
    qi = jnp.arange(C)[:, None]
    ki = jnp.arange(2 * C)[None, :]
    dist = qi + C - ki
    blk = jnp.arange(nb)[:, None, None]
    valid = (dist >= 0) & (dist <= steps) & (blk * C + ki - C >= 0)
    bias = -alibi_slopes()[:, None, None] * (dist * dil).astype(jnp.float32)
    s = jnp.where(valid[:, None], s + bias, NEG_INF)
    lse = jax.nn.logsumexp(s, axis=-1)
    p = jnp.exp(s - lse[..., None])
    o = jnp.einsum('brnhqk,brnkhd->brnqhd', p.astype(v.dtype), vb)
    o = o.reshape(B, dil, Lp, H, Dh)[:, :, :L].transpose(0, 2, 1, 3, 4).reshape(B, T, H, Dh)
    lse = lse.transpose(0, 1, 2, 4, 3).reshape(B, dil, Lp, H)[:, :, :L]
    lse = lse.transpose(0, 2, 1, 3).reshape(B, T, H)
    return o, lse


def combine_dilations(outs, lses, dtype):
    w = jax.nn.softmax(jnp.stack(lses, 0), axis=0)
    o = jnp.einsum('gbth,gbthd->bthd', w, jnp.stack(outs, 0).astype(jnp.float32))
    return o.astype(dtype)


def dilated_attention_prompt(q, k, v):
    outs, lses = [], []
    for window, dil in DILATIONS:
        o, lse = strided_band_attention(q, k, v, window, dil)
        outs.append(o)
        lses.append(lse)
    return combine_dilations(outs, lses, q.dtype)


def dilated_attention_decode(q, k_all, v_all, n_past):
    S = q.shape[1]
    outs, lses = [], []
    for window, dil in DILATIONS:
        steps = window // dil
        j = jnp.arange(steps + 1)
        idx = n_past + jnp.arange(S)[:, None] - j[None, :] * dil
        valid = idx >= 0
        idx = jnp.maximum(idx, 0)
        kg = k_all[:, idx]
        vg = v_all[:, idx]
        s = jnp.einsum('bshd,bsjhd->bhsj', q, kg,
                       preferred_element_type=jnp.float32) * (HEAD_DIM ** -0.5)
        s = s - alibi_slopes()[:, None, None] * (j * dil).astype(jnp.float32)[None, None, :]
        s = jnp.where(valid[None, None], s, NEG_INF)
        lse = jax.nn.logsumexp(s, axis=-1)
        p = jnp.exp(s - lse[..., None])
        outs.append(jnp.einsum('bhsj,bsjhd->bshd', p.astype(vg.dtype), vg))
        lses.append(lse.transpose(0, 2, 1))
    return combine_dilations(outs, lses, q.dtype)


def retention(q, k, v, s0):
    B, T, H, _ = q.shape
    dv = v.shape[-1]
    C = RET_CHUNK if T % RET_CHUNK == 0 else T
    n = T // C
    log_g = retention_log_gamma()

    def chunks(a):
        return a.astype(jnp.float32).reshape(B, n, C, H, a.shape[-1]).transpose(1, 0, 2, 3, 4)

    pos = jnp.arange(C, dtype=jnp.float32)
    rel = pos[:, None] - pos[None, :]
    decay_mask = jnp.where(rel >= 0, jnp.exp(jnp.maximum(rel, 0.0)[None] * log_g[:, None, None]), 0.0)
    q_decay = jnp.exp((pos[:, None] + 1.0) * log_g[None])
    k_decay = jnp.exp((C - 1.0 - pos[:, None]) * log_g[None])
    chunk_decay = jnp.exp(C * log_g)

    def step(state, inp):
        qc, kc, vc = inp
        inner = jnp.einsum('bihd,bjhd->bhij', qc, kc) * decay_mask
        o = (jnp.einsum('bhij,bjhe->bihe', inner, vc)
             + jnp.einsum('bihd,bhde->bihe', qc, state) * q_decay[None, :, :, None])
        state = (state * chunk_decay[None, :, None, None]
                 + jnp.einsum('bjhd,bjhe->bhde', kc * k_decay[None, :, :, None], vc))
        return state, o

    s_final, o = lax.scan(step, s0.astype(jnp.float32), (chunks(q), chunks(k), chunks(v)))
    return o.transpose(1, 0, 2, 3, 4).reshape(B, T, H, dv), s_final


def retention_readout(o, gb, gn_g, gn_b):
    B, T, H, dv = o.shape
    mu = o.mean(-1, keepdims=True)
    var = jnp.mean(jnp.square(o - mu), -1, keepdims=True)
    on = ((o - mu) * lax.rsqrt(var + GN_EPS)).reshape(B, T, H * dv)
    return ((on * gn_g + gn_b) * jax.nn.silu(gb.astype(jnp.float32))).astype(gb.dtype)


def merge_heads(oa, ob, w_out):
    B, T = oa.shape[:2]
    return jnp.concatenate([oa.reshape(B, T, WIDTH_A), ob], axis=-1) @ w_out


def mix_prompt(h, w_in, w_out, gn_g, gn_b):
    B, T, _ = h.shape
    qa, ka, va, qb, kb, vb, gb = split_projection(h, w_in)
    oa = dilated_attention_prompt(qa, ka, va)
    s0 = jnp.zeros((B, N_HEADS_B, DK_B, DV_B), jnp.float32)
    ob, s_final = retention(qb, kb, vb, s0)
    out = merge_heads(oa, retention_readout(ob, gb, gn_g, gn_b), w_out)
    keep = min(MAX_WINDOW, T)
    return out, (ka[:, T - keep:], va[:, T - keep:], s_final.astype(h.dtype))


def mix_sample(h, cache_k, cache_v, state, w_in, w_out, gn_g, gn_b):
    qa, ka, va, qb, kb, vb, gb = split_projection(h, w_in)
    n_past = cache_k.shape[1]
    k_all = jnp.concatenate([cache_k.astype(ka.dtype), ka], axis=1)
    v_all = jnp.concatenate([cache_v.astype(va.dtype), va], axis=1)
    oa = dilated_attention_decode(qa, k_all, v_all, n_past)
    ob, s_final = retention(qb, kb, vb, state)
    out = merge_heads(oa, retention_readout(ob, gb, gn_g, gn_b), w_out)
    return out, (ka, va, s_final.astype(h.dtype))


def decoder_layer(x, p, mix_fn, f1_w1, f1_w3, f1_w2, f2_w1, f2_w3, f2_w2, ln_g, ln_b, w_ple, w_gate):
    x = layer_norm(DEEPNORM_ALPHA * x + 0.5 * swiglu(x, f1_w1, f1_w3, f1_w2), ln_g[0], ln_b[0])
    m, new_state = mix_fn(x)
    x = layer_norm(DEEPNORM_ALPHA * x + m, ln_g[1], ln_b[1])
    x = layer_norm(DEEPNORM_ALPHA * x + 0.5 * swiglu(x, f2_w1, f2_w3, f2_w2), ln_g[2], ln_b[2])
    ple = (p @ w_ple) * jax.nn.sigmoid(x @ w_gate)
    x = layer_norm(DEEPNORM_ALPHA * x + ple, ln_g[3], ln_b[3])
    return x, new_state


def setup_inputs(seed: int = 0) -> dict:
    key = jax.random.key(seed)
    ks = jax.random.split(key, 24)
    f32 = jnp.float32
    nrm = lambda k, shape, s: jax.random.normal(k, shape, f32) * s
    win_buf = min(MAX_WINDOW, PAST_LEN)
    col_scale = np.ones((IN_WIDTH,), np.float32)
    col_scale[2 * WIDTH_A:3 * WIDTH_A] = DEEPNORM_BETA
    vb0 = 3 * WIDTH_A + 2 * WIDTH_BQK
    col_scale[vb0:vb0 + WIDTH_BV] = DEEPNORM_BETA
    return {
        "x_prompt": nrm(ks[0], (BATCH, SEQ, D_MODEL), 1.0),
        "x_sample": nrm(ks[1], (DEC_BATCH, DEC_SEQ, D_MODEL), 1.0),
        "cache_k": nrm(ks[2], (DEPTH, DEC_BATCH, win_buf, N_HEADS_A, HEAD_DIM), 1.0),
        "cache_v": nrm(ks[3], (DEPTH, DEC_BATCH, win_buf, N_HEADS_A, HEAD_DIM), DEEPNORM_BETA),
        "state_ret": nrm(ks[4], (DEPTH, DEC_BATCH, N_HEADS_B, DK_B, DV_B), 0.5),
        "p_prompt": nrm(ks[5], (DEPTH, BATCH, SEQ, PLE_DIM), 1.0),
        "p_sample": nrm(ks[6], (DEPTH, DEC_BATCH, DEC_SEQ, PLE_DIM), 1.0),
        "w_in": nrm(ks[7], (DEPTH, D_MODEL, IN_WIDTH), D_MODEL ** -0.5) * jnp.asarray(col_scale),
        "w_out": nrm(ks[8], (DEPTH, MIX_WIDTH, D_MODEL), MIX_WIDTH ** -0.5 * DEEPNORM_BETA),
        "gn_g": 1.0 + nrm(ks[9], (DEPTH, WIDTH_BV), 0.02),
        "gn_b": nrm(ks[10], (DEPTH, WIDTH_BV), 0.02),
        "ffn1_w1": nrm(ks[11], (DEPTH, D_MODEL, FFN_DIM), D_MODEL ** -0.5 * DEEPNORM_BETA),
        "ffn1_w3": nrm(ks[12], (DEPTH, D_MODEL, FFN_DIM), D_MODEL ** -0.5 * DEEPNORM_BETA),
        "ffn1_w2": nrm(ks[13], (DEPTH, FFN_DIM, D_MODEL), FFN_DIM ** -0.5 * DEEPNORM_BETA),
        "ffn2_w1": nrm(ks[14], (DEPTH, D_MODEL, FFN_DIM), D_MODEL ** -0.5 * DEEPNORM_BETA),
        "ffn2_w3": nrm(ks[15], (DEPTH, D_MODEL, FFN_DIM), D_MODEL ** -0.5 * DEEPNORM_BETA),
        "ffn2_w2": nrm(ks[16], (DEPTH, FFN_DIM, D_MODEL), FFN_DIM ** -0.5 * DEEPNORM_BETA),
        "w_ple": nrm(ks[17], (DEPTH, PLE_DIM, D_MODEL), PLE_DIM ** -0.5 * DEEPNORM_BETA),
        "w_gate": nrm(ks[18], (DEPTH, D_MODEL, D_MODEL), D_MODEL ** -0.5),
        "ln_g": 1.0 + nrm(ks[19], (DEPTH, 4, D_MODEL), 0.02),
        "ln_b": nrm(ks[20], (DEPTH, 4, D_MODEL), 0.02),
    }


def reference(x_prompt, x_sample, cache_k, cache_v, state_ret, p_prompt, p_sample,
              w_in, w_out, gn_g, gn_b, ffn1_w1, ffn1_w3, ffn1_w2, ffn2_w1, ffn2_w3, ffn2_w2,
              w_ple, w_gate, ln_g, ln_b):
    xp, xs = x_prompt, x_sample
    kp, vp, sp, ksm, vsm, ssm = [], [], [], [], [], []
    for i in range(DEPTH):
        layer_w = (ffn1_w1[i], ffn1_w3[i], ffn1_w2[i], ffn2_w1[i], ffn2_w3[i], ffn2_w2[i],
                   ln_g[i], ln_b[i], w_ple[i], w_gate[i])
        mix_p = functools.partial(mix_prompt, w_in=w_in[i], w_out=w_out[i],
                                  gn_g=gn_g[i], gn_b=gn_b[i])
        mix_s = functools.partial(mix_sample, cache_k=cache_k[i], cache_v=cache_v[i],
                                  state=state_ret[i], w_in=w_in[i], w_out=w_out[i],
                                  gn_g=gn_g[i], gn_b=gn_b[i])
        xp, (k_i, v_i, s_i) = decoder_layer(xp, p_prompt[i], mix_p, *layer_w)
        xs, (k_j, v_j, s_j) = decoder_layer(xs, p_sample[i], mix_s, *layer_w)
        kp.append(k_i); vp.append(v_i); sp.append(s_i)
        ksm.append(k_j); vsm.append(v_j); ssm.append(s_j)
    return (xp, xs, jnp.stack(kp), jnp.stack(vp), jnp.stack(sp),
            jnp.stack(ksm), jnp.stack(vsm), jnp.stack(ssm))
```

```python
import numpy as np
import ml_dtypes
from contextlib import ExitStack
import concourse.bass as bass
import concourse.mybir as mybir
from concourse.bass_utils import run_bass_kernel_spmd

F32 = mybir.dt.float32
BF16 = mybir.dt.bfloat16
AF = mybir.ActivationFunctionType
ALU = mybir.AluOpType
AX = mybir.AxisListType

DEPTH = 4
D = 2048
NCH = 16
T = 1024
TC = 1025
TP = 1028
FFN = 5632
NG = 22
ALPHA = float((2 * DEPTH) ** 0.25)
LN_EPS = 1e-5
GN_EPS = 1e-5
TTS = [(0, 342), (342, 342), (684, 341)]
NEG = -30000.0
QSCALE = float(128 ** -0.5)
RSZ = 1032
NREG = 14
NSLOT = 4
DBG = {}


class Eng:
    def __init__(self, name):
        self.name = name
        self.ops = []
        self.sem = None
        self.count = 0
        self.waited = {}
        self.extra = []


class DSem:
    def __init__(self, handle):
        self.h = handle
        self.count = 0


class Buf:
    def __init__(self, name):
        self.name = name
        self.w = {}
        self.r = {}


class Prog:
    def __init__(self):
        self.PE = Eng("pe")
        self.ACT = Eng("act")
        self.DVE = Eng("dve")
        self.POOL = Eng("pool")
        self.SP = Eng("sp")
        self.engs = [self.PE, self.ACT, self.DVE, self.POOL, self.SP]
        self.dsems = []
        self.pool_i = 0

    def _waits(self, eng, reads, writes, own):
        waits = {}

        def need(sem, val):
            if eng.waited.get(sem, 0) < val:
                if waits.get(sem, 0) < val:
                    waits[sem] = val

        for b in reads:
            for k, (s, v) in b.w.items():
                need(s, v)
        for b in writes:
            for k, (s, v) in b.w.items():
                if k != own and not (own == "dma" and k.startswith("dma")):
                    need(s, v)
            for k, (s, v) in b.r.items():
                if k != own:
                    need(s, v)
        for (s, v) in eng.extra:
            need(s, v)
        eng.extra = []
        for s, v in waits.items():
            eng.waited[s] = v
        return list(waits.items())

    def op(self, eng, fn, reads=(), writes=()):
        waits = self._waits(eng, reads, writes, eng.name)
        eng.count += 1
        my = (eng.sem, eng.count)
        for b in reads:
            b.r[eng.name] = my
        for b in writes:
            b.w = {eng.name: my}
            b.r = {}
        eng.ops.append((waits, fn, (eng.sem, 1)))

    def dma(self, eng, fn, dsem, reads=(), writes=()):
        fns = fn if isinstance(fn, list) else [fn]
        if isinstance(dsem, list):
            pool = dsem
            dsem = pool[self.pool_i % len(pool)]
            self.pool_i += 1
        key = "dma%d" % id(dsem)
        waits = self._waits(eng, reads, writes, "dma")
        if dsem.count > 0 and eng.waited.get(dsem.h, 0) < dsem.count:
            waits.append((dsem.h, dsem.count))
            eng.waited[dsem.h] = dsem.count
        for i, f in enumerate(fns):
            dsem.count += 16
            eng.ops.append((waits if i == 0 else [], f, (dsem.h, 16)))
        my = (dsem.h, dsem.count)
        for b in reads:
            b.r[key] = my
        for b in writes:
            b.w[key] = my
            b.r = {}

    def replay(self, eng, e):
        for waits, fn, inc in eng.ops:
            for s, v in waits:
                e.wait_ge(s, v)
            ins = fn(e)
            ins.then_inc(inc[0], inc[1])


def _consts():
    c = {}
    c["ident_bf"] = np.eye(128, dtype=np.float32).astype(ml_dtypes.bfloat16)
    c["ones_bf"] = np.ones((128, 128), np.float32).astype(ml_dtypes.bfloat16)
    slopes = np.exp2(-8.0 * np.arange(1, 9) / 8.0)
    j = np.arange(128)[:, None]
    i = np.arange(128)[None, :]
    bias = np.zeros((128, 12, 2, 128), np.float32)
    for ei in range(12):
        sd = 2.0 ** (ei - 8)
        bias[:, ei, 0, :] = np.where(j <= i, -sd * (i - j), NEG)
        bias[:, ei, 1, :] = np.where(j >= i, -sd * (i + 128 - j), NEG)
    c["biasT"] = bias.reshape(128, -1).astype(ml_dtypes.bfloat16)
    lg = np.log(1.0 - np.exp2(-5.0 - np.arange(8, dtype=np.float64)))
    dec = np.zeros((128, 8, 128), np.float64)
    for h in range(8):
        dec[:, h, :] = np.where(i >= j, np.exp((i - j) * lg[h]), 0.0)
    c["decayT"] = dec.reshape(128, -1).astype(np.float32)
    hp = np.arange(128) // 64
    small = np.zeros((128, 64), np.float64)
    qdec = np.zeros((128, 4, 128), np.float64)
    for cp in range(4):
        hh = 2 * cp + hp
        qdec[:, cp, :] = np.exp((np.arange(128)[None, :] + 1.0) * lg[hh][:, None])
        small[:, 0 + cp] = np.exp(128.0 * lg[hh])
        small[:, 4 + cp] = np.exp(lg[hh])
    for h in range(8):
        small[:, 8 + h] = np.exp((127.0 - np.arange(128)) * lg[h]) / 8.0
    small[:, 16] = LN_EPS
    small[:, 17] = GN_EPS
    small[:, 18] = np.log(3.0)
    for h in range(8):
        for g, dil in enumerate((1, 4, 16)):
            small[:, 20 + h * 3 + g] = -slopes[h] * (128.0 - np.arange(128)) * dil
    c["qdec"] = qdec.reshape(128, -1).astype(np.float32)
    c["small"] = small.astype(np.float32)
    return c


CONST_SHAPES = {
    "ident_bf": ([128, 128], BF16), "ones_bf": ([128, 128], BF16), "biasT": ([128, 3072], BF16),
    "decayT": ([128, 1024], F32), "qdec": ([128, 512], F32), "small": ([128, 64], F32),
}


def build(n_layers=DEPTH, do_mix=True, do_ffn=True, do_ple=True, exchange=False, n_pass=2, stop_after=None):
    nc = bass.Bass("TRN2", target_bir_lowering=False)
    P = Prog()
    PE, ACT, DVE, POOL, SP = P.PE, P.ACT, P.DVE, P.POOL, P.SP

    def din(name, shape, dt=F32):
        return nc.dram_tensor(name, shape, dt, kind="ExternalInput").ap()

    def dout(name, shape, dt=F32):
        return nc.dram_tensor(name, shape, dt, kind="ExternalOutput").ap()

    xT_d = din("xT", [2, D, TC])
    pT_d = din("pT", [2, DEPTH, 256, TC])
    ck_d = din("ck", [DEPTH, 2048, 1024])
    cv_d = din("cv", [DEPTH, 2048, 1024])
    st_d = din("st", [DEPTH, 4, 128, 128])
    w_in_d = din("w_in", [DEPTH, D, 6144])
    w_out_d = din("w_out", [DEPTH, D, D])
    f_w1 = [din("ffn1_w1", [DEPTH, D, FFN]), din("ffn2_w1", [DEPTH, D, FFN])]
    f_w3 = [din("ffn1_w3", [DEPTH, D, FFN]), din("ffn2_w3", [DEPTH, D, FFN])]
    f_w2 = [din("ffn1_w2", [DEPTH, FFN, D]), din("ffn2_w2", [DEPTH, FFN, D])]
    w_ple_d = din("w_ple", [DEPTH, 256, D])
    w_gate_d = din("w_gate", [DEPTH, D, D])
    lng_d = din("lng", [128, DEPTH * 4 * NCH])
    lnb_d = din("lnb", [128, DEPTH * 4 * NCH])
    gng_d = din("gng", [128, DEPTH * 8])
    gnb_d = din("gnb", [128, DEPTH * 8])
    cdram = {k: din(k, s, dt) for k, (s, dt) in CONST_SHAPES.items()}

    yT_o = dout("yT", [2, D, TC])
    kT_o = dout("kTo", [2, DEPTH, 8, 128, T])
    vT_o = dout("vTo", [2, DEPTH, 8, 128, T])
    ks_o = dout("kso", [128, DEPTH * 8])
    vs_o = dout("vso", [128, DEPTH * 8])
    rp_o = dout("rpo", [2, DEPTH, 4, 128, 128])
    kTs_d = nc.dram_tensor("kTs", [DEPTH, 8, 128, T], BF16).ap()
    vTs_d = nc.dram_tensor("vTs", [DEPTH, 8, 128, T], BF16).ap()
    Sps_d = nc.dram_tensor("Sps", [DEPTH, 4, 128, 128], F32).ap()
    rs_o = dout("rso", [DEPTH, 4, 128, 128])

    es = ExitStack()
    with es:
        def sb(name, shape, dt):
            return es.enter_context(nc.sbuf_tensor("sb_" + name, shape, dt))

        x32 = sb("x32", [128, NCH, TP], F32)
        xbf = sb("xbf", [128, NCH, TP], BF16)
        slots = [sb("slot%d" % i, [128, 4096], BF16) for i in range(NSLOT)]
        regs = [sb("reg%d" % i, [128, RSZ], F32) for i in range(NREG)]
        ctile = {k: sb("c_" + k, s, dt) for k, (s, dt) in CONST_SHAPES.items()}
        lng = sb("lng", [128, DEPTH * 4 * NCH], F32)
        lnb = sb("lnb", [128, DEPTH * 4 * NCH], F32)
        lngA = sb("lngA", [128, DEPTH * 4 * NCH], F32)
        lnbA = sb("lnbA", [128, DEPTH * 4 * NCH], F32)
        gng = sb("gng", [128, DEPTH * 8], F32)
        gnb = sb("gnb", [128, DEPTH * 8], F32)
        ks32 = sb("ks32", [128, DEPTH * 8], F32)
        vs32 = sb("vs32", [128, DEPTH * 8], F32)
        pTs = sb("pTs", [128, 2, TP], BF16)
        psums = [es.enter_context(nc.psum_tensor("ps%d" % i, [128, 512], F32)) for i in range(8)]
        for e_ in P.engs:
            e_.sem = es.enter_context(nc.semaphore("sem_" + e_.name))
        slot_sems = [[DSem(es.enter_context(nc.semaphore("ssem%d_%d" % (i, k_)))) for k_ in range(2)] for i in range(NSLOT)]
        sp_pool = [DSem(es.enter_context(nc.semaphore("spsem%d" % i))) for i in range(24)]
        sem_in = sp_pool
        sem_in2 = sp_pool
        sem_out = sp_pool

        B_x32 = [[Buf("x32_%d_%d" % (c, t)) for t in range(3)] for c in range(NCH)]
        B_xbf = [Buf("xbf_%d" % t) for t in range(3)]
        B_slot = [Buf("slot%d" % i) for i in range(NSLOT)]
        B_reg = [Buf("reg%d" % i) for i in range(NREG)]
        B_ps = [Buf("ps%d" % i) for i in range(8)]
        B_const = Buf("const")
        B_ks = Buf("ks32")
        B_pT = Buf("pT")
        B_kvs = [[Buf("kvs%d_%d" % (l_, h_)) for h_ in range(8)] for l_ in range(DEPTH)]
        B_sps = [[Buf("sps%d_%d" % (l_, c_)) for c_ in range(4)] for l_ in range(DEPTH)]
        cur = {"pj": 0}

        def xbf_bufs(a, b):
            return tuple(B_xbf[ti] for ti, (t0, tn) in enumerate(TTS) if t0 < b and a < t0 + tn)
        ps_state = {"i": 0, "held": set()}

        def ps_next():
            while True:
                i = ps_state["i"]
                ps_state["i"] = (i + 1) % 8
                if i not in ps_state["held"]:
                    return i

        ring = {"i": 0}

        def slot_load(parts):
            i = ring["i"]
            ring["i"] = (i + 1) % NSLOT
            for pi_, (dstf, src) in enumerate(parts):
                dst = dstf(slots[i])
                P.dma(POOL, (lambda e, dst=dst, src=src: e.dma_start(out=dst, in_=src)), slot_sems[i][pi_],
                      reads=(), writes=(B_slot[i],))
            return i

        def v3(t, k):
            return t[:, 0:16 * k].rearrange("p (c n) -> p c n", c=16)

        def wcols(wd, l, c0, n):
            return wd[l].rearrange("(kc p) n -> p kc n", p=128)[:, :, c0:c0 + n]

        cst = {k: ctile[k] for k in ctile}
        small = cst["small"]
        ones_bf = cst["ones_bf"]
        ident_bf = cst["ident_bf"]

        def rf(i, a=0, n=RSZ):
            return regs[i][:, a:a + n]

        def rb(i, a=0, n=2 * RSZ):
            return regs[i][:, :].bitcast(BF16)[:, a:a + n]

        for k in ctile:
            P.dma(SP, (lambda e, k=k: e.dma_start(out=ctile[k][:, :], in_=cdram[k][:, :])),
                  sem_in, writes=(B_const,))
        for (tl, dd) in ((lng, lng_d), (lnb, lnb_d), (gng, gng_d), (gnb, gnb_d)):
            P.dma(SP, (lambda e, tl=tl, dd=dd: e.dma_start(out=tl[:, :], in_=dd[:, :])),
                  sem_in, writes=(B_const,))
        P.op(DVE, lambda e: e.tensor_scalar(out=lngA[:, :], in0=lng[:, :], scalar1=ALPHA, scalar2=None,
                                            op0=ALU.mult), reads=(B_const,), writes=(B_const,))
        P.op(DVE, lambda e: e.tensor_scalar(out=lnbA[:, :], in0=lnb[:, :], scalar1=ALPHA, scalar2=None,
                                            op0=ALU.mult), reads=(B_const,), writes=(B_const,))
        P.op(DVE, lambda e: e.memset(ks32[:, :], 0.0), writes=(B_ks,))
        P.op(DVE, lambda e: e.memset(vs32[:, :], 0.0), writes=(B_ks,))
        def load_x(pj):
            xT_v = xT_d[pj].rearrange("(c p) t -> p c t", p=128)
            for c in range(NCH):
                P.dma(SP, (lambda e, c=c: e.dma_start(out=x32[:, c, 0:TC], in_=xT_v[:, c, :])),
                      sem_in2, writes=tuple(B_x32[c]))
            for c in range(NCH):
                for ti, (t0, tn) in enumerate(TTS):
                    P.op(ACT, (lambda e, c=c, t0=t0, tn=tn: e.activation(
                        out=xbf[:, c, t0:t0 + tn], in_=x32[:, c, t0:t0 + tn], func=AF.Copy)),
                        reads=(B_x32[c][ti],), writes=(B_xbf[ti],))
                    P.op(DVE, (lambda e, c=c, t0=t0, tn=tn: e.tensor_scalar(
                        out=x32[:, c, t0:t0 + tn], in0=x32[:, c, t0:t0 + tn], scalar1=ALPHA, scalar2=None,
                        op0=ALU.mult)), reads=(B_x32[c][ti],), writes=(B_x32[c][ti],))

        def mm_group(out_ap, pairs, reads, writes):
            n = len(pairs)

            def fn(e):
                ins = None
                for i, (l, r) in enumerate(pairs):
                    ins = e.matmul(out_ap, l, r, start=(i == 0), stop=(i == n - 1))
                return ins
            P.op(PE, fn, reads=reads, writes=writes)

        def layer_norm(lidx, final=False):
            R_lnP = (10, 13)
            R_st = 11
            R_uP = (12, 9)
            for ti, (t0, tn) in enumerate(TTS):
                b1 = ps_next(); ps_state["held"].add(b1)
                b2 = ps_next(); ps_state["held"].add(b2)
                for c in range(NCH):
                    R_ln = R_lnP[c % 2]
                    lv = rb(R_ln, 0, tn)
                    lq = rb(R_ln, 1024, tn)
                    P.op(ACT, (lambda e, c=c, lv=lv, t0=t0, tn=tn: e.activation(
                        out=lv, in_=x32[:, c, t0:t0 + tn], func=AF.Copy)),
                        reads=(B_x32[c][ti],), writes=(B_reg[R_ln],))
                    P.op(ACT, (lambda e, c=c, lq=lq, t0=t0, tn=tn: e.activation(
                        out=lq, in_=x32[:, c, t0:t0 + tn], func=AF.Square)),
                        reads=(B_x32[c][ti],), writes=(B_reg[R_ln],))

                    def fn(e, c=c, lv=lv, lq=lq, tn=tn, b1=b1, b2=b2):
                        e.matmul(psums[b1][:, 0:tn], ones_bf[:, :], lv, start=(c == 0), stop=(c == NCH - 1))
                        return e.matmul(psums[b2][:, 0:tn], ones_bf[:, :], lq, start=(c == 0),
                                        stop=(c == NCH - 1))
                    P.op(PE, fn, reads=(B_reg[R_ln], B_const), writes=(B_ps[b1], B_ps[b2]))
                mean = rf(R_st, 0, tn)
                rstd = rf(R_st, 512, tn)
                P.op(DVE, (lambda e, mean=mean, b1=b1, tn=tn: e.tensor_scalar(
                    out=mean, in0=psums[b1][:, 0:tn], scalar1=1.0 / D, scalar2=None, op0=ALU.mult)),
                    reads=(B_ps[b1],), writes=(B_reg[R_st],))
                P.op(DVE, (lambda e, mean=mean, rstd=rstd: e.tensor_tensor(
                    out=rstd, in0=mean, in1=mean, op=ALU.mult)),
                    reads=(B_reg[R_st],), writes=(B_reg[R_st],))
                P.op(DVE, (lambda e, rstd=rstd, b2=b2, tn=tn: e.scalar_tensor_tensor(
                    out=rstd, in0=psums[b2][:, 0:tn], scalar=1.0 / D, in1=rstd, op0=ALU.mult,
                    op1=ALU.subtract)), reads=(B_ps[b2], B_reg[R_st]), writes=(B_reg[R_st],))
                ps_state["held"].discard(b1); ps_state["held"].discard(b2)
                P.op(ACT, (lambda e, rstd=rstd: e.activation(
                    out=rstd, in_=rstd, func=AF.Sqrt, bias=small[:, 16:17], scale=1.0)),
                    reads=(B_reg[R_st], B_const), writes=(B_reg[R_st],))
                P.op(DVE, (lambda e, rstd=rstd: e.reciprocal(out=rstd, in_=rstd)),
                     reads=(B_reg[R_st],), writes=(B_reg[R_st],))
                for c in range(NCH):
                    R_u = R_uP[c % 2]
                    u = rf(R_u, 0, tn)
                    col = lidx * NCH + c
                    P.op(DVE, (lambda e, u=u, c=c, t0=t0, tn=tn, mean=mean: e.tensor_tensor(
                        out=u, in0=x32[:, c, t0:t0 + tn], in1=mean, op=ALU.subtract)),
                        reads=(B_x32[c][ti], B_reg[R_st]), writes=(B_reg[R_u],))
                    P.op(DVE, (lambda e, u=u, rstd=rstd: e.tensor_tensor(
                        out=u, in0=u, in1=rstd, op=ALU.mult)),
                        reads=(B_reg[R_u], B_reg[R_st]), writes=(B_reg[R_u],))
                    P.op(ACT, (lambda e, u=u, c=c, col=col, t0=t0, tn=tn: e.activation(
                        out=xbf[:, c, t0:t0 + tn], in_=u, func=AF.Identity,
                        bias=lnb[:, col:col + 1], scale=lng[:, col:col + 1])),
                        reads=(B_reg[R_u], B_const), writes=(B_xbf[ti],))
                    if final:
                        P.op(ACT, (lambda e, u=u, c=c, col=col, t0=t0, tn=tn: e.activation(
                            out=x32[:, c, t0:t0 + tn], in_=u, func=AF.Identity,
                            bias=lnb[:, col:col + 1], scale=lng[:, col:col + 1])),
                            reads=(B_reg[R_u], B_const), writes=(B_x32[c][ti],))
                    else:
                        P.op(ACT, (lambda e, u=u, c=c, col=col, t0=t0, tn=tn: e.activation(
                            out=x32[:, c, t0:t0 + tn], in_=u, func=AF.Identity,
                            bias=lnbA[:, col:col + 1], scale=lngA[:, col:col + 1])),
                            reads=(B_reg[R_u], B_const), writes=(B_x32[c][ti],))

        def ffn(l, which):
            w1d, w3d, w2d = f_w1[which], f_w3[which], f_w2[which]
            R_h = (0, 1)
            R_sa = 2
            for g in range(NG):
                s1 = slot_load([(lambda t: v3(t, 256), wcols(w1d, l, g * 256, 256))])
                s3 = slot_load([(lambda t: v3(t, 256), wcols(w3d, l, g * 256, 256))])
                s2 = slot_load([(lambda t: t[:, :].rearrange("p (h n) -> p h n", h=2),
                                 w2d[l][g * 256:(g + 1) * 256, :].rearrange("(h p) n -> p h n", p=128))])
                W1 = v3(slots[s1], 256); W3 = v3(slots[s3], 256)
                W2 = slots[s2][:, :].rearrange("p (h n) -> p h n", h=2)
                rh = R_h[g % 2]
                hb = rb(rh, 0, 2 * TP).rearrange("p (h t) -> p h t", h=2)
                for hc in range(2):
                    for ti, (t0, tn) in enumerate(TTS):
                        ba = ps_next(); bb = ps_next()
                        mm_group(psums[ba][:, 0:tn],
                                 [(W1[:, kc, hc * 128:(hc + 1) * 128], xbf[:, kc, t0:t0 + tn]) for kc in range(NCH)],
                                 reads=(B_slot[s1], B_xbf[ti]), writes=(B_ps[ba],))
                        mm_group(psums[bb][:, 0:tn],
                                 [(W3[:, kc, hc * 128:(hc + 1) * 128], xbf[:, kc, t0:t0 + tn]) for kc in range(NCH)],
                                 reads=(B_slot[s3], B_xbf[ti]), writes=(B_ps[bb],))
                        sa = rf(R_sa, (ti % 2) * 512, tn)
                        P.op(ACT, (lambda e, sa=sa, ba=ba, tn=tn: e.activation(
                            out=sa, in_=psums[ba][:, 0:tn], func=AF.Silu)),
                            reads=(B_ps[ba],), writes=(B_reg[R_sa],))
                        P.op(DVE, (lambda e, sa=sa, bb=bb, tn=tn, hb=hb, hc=hc, t0=t0: e.tensor_tensor(
                            out=hb[:, hc, t0:t0 + tn], in0=psums[bb][:, 0:tn], in1=sa, op=ALU.mult)),
                            reads=(B_ps[bb], B_reg[R_sa]), writes=(B_reg[rh],))
                for oc in range(NCH):
                    for ti, (t0, tn) in enumerate(TTS):
                        by = ps_next()
                        mm_group(psums[by][:, 0:tn],
                                 [(W2[:, hc, oc * 128:(oc + 1) * 128], hb[:, hc, t0:t0 + tn]) for hc in range(2)],
                                 reads=(B_slot[s2], B_reg[rh]), writes=(B_ps[by],))
                        P.op(DVE, (lambda e, by=by, oc=oc, t0=t0, tn=tn: e.scalar_tensor_tensor(
                            out=x32[:, oc, t0:t0 + tn], in0=psums[by][:, 0:tn], scalar=0.5,
                            in1=x32[:, oc, t0:t0 + tn], op0=ALU.mult, op1=ALU.add)),
                            reads=(B_ps[by], B_x32[oc][ti]), writes=(B_x32[oc][ti],))

        def ple(l):
            R_p = 3
            for kc in range(2):
                pv = rf(R_p, 0, TC)
                P.dma(SP, (lambda e, kc=kc, pv=pv, pj=cur['pj']: e.dma_start(out=pv, in_=pT_d[pj, l, kc * 128:(kc + 1) * 128, :])),
                      sem_in, writes=(B_reg[R_p],))
                P.op(ACT, (lambda e, kc=kc, pv=pv: e.activation(out=pTs[:, kc, 0:TC], in_=pv, func=AF.Copy)),
                     reads=(B_reg[R_p],), writes=(B_pT,))
            R_sg = 4
            for og in range(8):
                sg_ = slot_load([(lambda t: v3(t, 256), wcols(w_gate_d, l, og * 256, 256))])
                WG = v3(slots[sg_], 256)
                sp = slot_load([(lambda t: t[:, 0:512].rearrange("p (h n) -> p h n", h=2),
                                 w_ple_d[l].rearrange("(h p) n -> p h n", p=128)[:, :, og * 256:(og + 1) * 256])])
                WP = slots[sp][:, 0:512].rearrange("p (h n) -> p h n", h=2)
                for o2 in range(2):
                    oc = og * 2 + o2
                    for ti, (t0, tn) in enumerate(TTS):
                        bg = ps_next(); bp = ps_next()
                        mm_group(psums[bg][:, 0:tn],
                                 [(WG[:, kc, o2 * 128:(o2 + 1) * 128], xbf[:, kc, t0:t0 + tn]) for kc in range(NCH)],
                                 reads=(B_slot[sg_], B_xbf[ti]), writes=(B_ps[bg],))
                        mm_group(psums[bp][:, 0:tn],
                                 [(WP[:, kc, o2 * 128:(o2 + 1) * 128], pTs[:, kc, t0:t0 + tn]) for kc in range(2)],
                                 reads=(B_slot[sp], B_pT), writes=(B_ps[bp],))
                        sgv = rf(R_sg, (ti % 2) * 512, tn)
                        P.op(ACT, (lambda e, sgv=sgv, bg=bg, tn=tn: e.activation(
                            out=sgv, in_=psums[bg][:, 0:tn], func=AF.Sigmoid)),
                            reads=(B_ps[bg],), writes=(B_reg[R_sg],))
                        P.op(DVE, (lambda e, sgv=sgv, bp=bp, tn=tn: e.tensor_tensor(
                            out=sgv, in0=psums[bp][:, 0:tn], in1=sgv, op=ALU.mult)),
                            reads=(B_ps[bp], B_reg[R_sg]), writes=(B_reg[R_sg],))
                        P.op(DVE, (lambda e, sgv=sgv, oc=oc, t0=t0, tn=tn: e.tensor_tensor(
                            out=x32[:, oc, t0:t0 + tn], in0=x32[:, oc, t0:t0 + tn], in1=sgv, op=ALU.add)),
                            reads=(B_reg[R_sg], B_x32[oc][ti]), writes=(B_x32[oc][ti],))

        def wout_accum(l, sw, rows_view, mixc_views, r_mix):
            for oc in range(NCH):
                for ti, (t0, tn) in enumerate(TTS):
                    by = ps_next()
                    mm_group(psums[by][:, 0:tn],
                             [(rows_view[k][:, oc * 128:(oc + 1) * 128], mixc_views[k][:, t0:t0 + tn])
                              for k in range(len(rows_view))],
                             reads=(B_slot[sw],) + tuple(B_reg[r] for r in r_mix), writes=(B_ps[by],))
                    P.op(DVE, (lambda e, by=by, oc=oc, t0=t0, tn=tn: e.tensor_tensor(
                        out=x32[:, oc, t0:t0 + tn], in0=psums[by][:, 0:tn], in1=x32[:, oc, t0:t0 + tn],
                        op=ALU.add)), reads=(B_ps[by], B_x32[oc][ti]), writes=(B_x32[oc][ti],))

        def a_head(l, h):
            R_qk, R_k32, R_v32, R_vp, R_vd, R_vo, R_N, R_D, R_s = 0, 1, 2, 3, 4, 5, 6, 7, 8
            sx = slot_load([(lambda t: t[:, 0:2048].rearrange("p (c n) -> p c n", c=16), wcols(w_in_d, l, h * 128, 128)),
                            (lambda t: t[:, 2048:4096].rearrange("p (c n) -> p c n", c=16),
                             wcols(w_in_d, l, 1024 + h * 128, 128))])
            sy = slot_load([(lambda t: t[:, 0:2048].rearrange("p (c n) -> p c n", c=16),
                             wcols(w_in_d, l, 2048 + h * 128, 128)),
                            (lambda t: t[:, 2048:4096], w_out_d[l][h * 128:(h + 1) * 128, :])])
            WXq = slots[sx][:, 0:2048].rearrange("p (c n) -> p c n", c=16)
            WXk = slots[sx][:, 2048:4096].rearrange("p (c n) -> p c n", c=16)
            WV = slots[sy][:, 0:2048].rearrange("p (c n) -> p c n", c=16)
            WO = slots[sy][:, 2048:4096]
            qT = rb(R_qk, 0, TP); kT = rb(R_qk, TP, TP)
            k32 = rf(R_k32, 0, TP); v32 = rf(R_v32, 0, TP)
            vTb = rb(R_vp, 0, TP)
            PTa = rb(R_vp, TP, 512); PTb = rb(R_vp, TP + 512, 512)
            Vd1 = rb(R_vd, 0, 1024).rearrange("p (n d) -> p n d", n=8)
            Vd4 = rb(R_vd, 1024, 1024).rearrange("p (n d) -> p n d", n=8)
            outc = rb(R_vo, 1024, TP)
            Nsb = rf(R_N, 0, TP); Dsb = rf(R_D, 0, TP)
            col = l * 8 + h
            for ti, (t0, tn) in enumerate(TTS):
                bq = ps_next()
                mm_group(psums[bq][:, 0:tn], [(WXq[:, kc, :], xbf[:, kc, t0:t0 + tn]) for kc in range(NCH)],
                         reads=(B_slot[sx], B_xbf[ti]), writes=(B_ps[bq],))
                P.op(DVE, (lambda e, bq=bq, t0=t0, tn=tn: e.tensor_scalar(
                    out=qT[:, t0:t0 + tn], in0=psums[bq][:, 0:tn], scalar1=QSCALE, scalar2=None, op0=ALU.mult)),
                    reads=(B_ps[bq],), writes=(B_reg[R_qk],))
                bk = ps_next()
                mm_group(psums[bk][:, 0:tn], [(WXk[:, kc, :], xbf[:, kc, t0:t0 + tn]) for kc in range(NCH)],
                         reads=(B_slot[sx], B_xbf[ti]), writes=(B_ps[bk],))
                P.op(ACT, (lambda e, bk=bk, t0=t0, tn=tn: e.activation(
                    out=k32[:, t0:t0 + tn], in_=psums[bk][:, 0:tn], func=AF.Copy)),
                    reads=(B_ps[bk],), writes=(B_reg[R_k32],))
                P.op(DVE, (lambda e, t0=t0, tn=tn: e.tensor_copy(
                    out=kT[:, t0:t0 + tn], in_=k32[:, t0:t0 + tn])),
                    reads=(B_reg[R_k32],), writes=(B_reg[R_qk],))
                bv = ps_next()
                mm_group(psums[bv][:, 0:tn], [(WV[:, kc, :], xbf[:, kc, t0:t0 + tn]) for kc in range(NCH)],
                         reads=(B_slot[sy], B_xbf[ti]), writes=(B_ps[bv],))
                P.op(ACT, (lambda e, bv=bv, t0=t0, tn=tn: e.activation(
                    out=v32[:, t0:t0 + tn], in_=psums[bv][:, 0:tn], func=AF.Copy)),
                    reads=(B_ps[bv],), writes=(B_reg[R_v32],))
                P.op(DVE, (lambda e, t0=t0, tn=tn: e.tensor_copy(
                    out=vTb[:, t0:t0 + tn], in_=v32[:, t0:t0 + tn])),
                    reads=(B_reg[R_v32],), writes=(B_reg[R_vp],))
            P.dma(SP, (lambda e, pj=cur['pj']: e.dma_start(out=kT_o[pj, l, h, :, :], in_=k32[:, 0:T])), sem_out,
                  reads=(B_reg[R_k32],))
            P.dma(SP, (lambda e, pj=cur['pj']: e.dma_start(out=vT_o[pj, l, h, :, :], in_=v32[:, 0:T])), sem_out,
                  reads=(B_reg[R_v32],))
            has_pred = (cur['pj'] == 1)
            if cur['pj'] == 0 and n_pass == 2:
                P.dma(SP, [(lambda e: e.dma_start(out=kTs_d[l, h, :, :], in_=kT[:, 0:T])),
                           (lambda e: e.dma_start(out=vTs_d[l, h, :, :], in_=vTb[:, 0:T]))], sem_out,
                      reads=(B_reg[R_qk], B_reg[R_vp]), writes=(B_kvs[l][h],))
            P.op(DVE, (lambda e: e.tensor_copy(out=ks32[:, col:col + 1], in_=k32[:, T:TC])),
                 reads=(B_reg[R_k32],), writes=(B_ks,))
            P.op(DVE, (lambda e: e.tensor_copy(out=vs32[:, col:col + 1], in_=v32[:, T:TC])),
                 reads=(B_reg[R_v32],), writes=(B_ks,))
            def vsrc1(n): return vTb[:, n * 128:(n + 1) * 128]
            def vsrc4(i):
                r, n = i // 2, i % 2
                return vTb[:, 512 * n + r: 512 * n + r + 509: 4]
            def vsrc16(r): return vTb[:, r: r + 1009: 16]
            for (dst, srcf, cnt, w) in ((Vd1, vsrc1, 8, 128), (Vd4, vsrc4, 8, 128)):
                for half in range(2):
                    bt = ps_next()
                    pst = psums[bt][:, :].bitcast(BF16)

                    def fn(e, half=half, pst=pst, srcf=srcf):
                        ins = None
                        for q in range(4):
                            ins = e.transpose(pst[:, q * 128:(q + 1) * 128], srcf(half * 4 + q), ident_bf[:, :])
                        return ins
                    P.op(PE, fn, reads=(B_reg[R_vp], B_const), writes=(B_ps[bt],))
                    P.op(DVE, (lambda e, dst=dst, half=half, pst=pst: e.tensor_copy(
                        out=dst[:, half * 4:(half + 1) * 4, :],
                        in_=pst[:, 0:512].rearrange("p (n d) -> p n d", n=4))),
                        reads=(B_ps[bt],), writes=(B_reg[R_vd],))
            Vd16 = rb(R_s, 0, 2048).rearrange("p (r d) -> p r d", r=16)
            for quarter in range(2):
                bt = ps_next()
                pst = psums[bt][:, :].bitcast(BF16)

                def fn(e, quarter=quarter, pst=pst):
                    ins = None
                    for q in range(8):
                        ins = e.transpose(pst[0:64, q * 128:(q + 1) * 128], vsrc16(quarter * 8 + q), ident_bf[:, :])
                    return ins
                P.op(PE, fn, reads=(B_reg[R_vp], B_const), writes=(B_ps[bt],))
                P.op(ACT, (lambda e, quarter=quarter, pst=pst: e.activation(
                    out=Vd16[0:64, quarter * 8:(quarter + 1) * 8, :],
                    in_=pst[0:64, 0:1024].rearrange("p (r d) -> p r d", r=8), func=AF.Copy)),
                    reads=(B_ps[bt],), writes=(B_reg[R_s],))
            biasT = cst["biasT"][:, :].rearrange("p (x o q) -> p x o q", x=12, o=2)
            R_pk, R_pv, R_pv2 = 10, 11, 12
            kTp = rb(R_pk, 0, T); vTp = rb(R_pk, 1024, T)
            Vp16 = rb(R_pv, 0, 2048).rearrange("p (r d) -> p r d", r=16)
            Vp1 = rb(R_pv2, 0, 128)
            Vp4 = rb(R_pv2, 128, 512).rearrange("p (r d) -> p r d", r=4)
            if has_pred:
                P.dma(SP, [(lambda e: e.dma_start(out=kTp, in_=kTs_d[l, h, :, :])),
                           (lambda e: e.dma_start(out=vTp, in_=vTs_d[l, h, :, :]))], sem_in,
                      reads=(B_kvs[l][h],), writes=(B_reg[R_pk],))
                bt = ps_next()
                pst = psums[bt][:, :].bitcast(BF16)

                def fnp(e, pst=pst):
                    e.transpose(pst[:, 0:128], vTp[:, 896:1024], ident_bf[:, :])
                    ins = None
                    for r in range(4):
                        ins = e.transpose(pst[:, 128 + r * 128:256 + r * 128], vTp[:, 512 + r:512 + r + 509:4],
                                          ident_bf[:, :])
                    return ins
                P.op(PE, fnp, reads=(B_reg[R_pk], B_const), writes=(B_ps[bt],))
                P.op(DVE, (lambda e, pst=pst: e.tensor_copy(out=rb(R_pv2, 0, 640), in_=pst[:, 0:640])),
                     reads=(B_ps[bt],), writes=(B_reg[R_pv2],))
                for quarter in range(2):
                    bt = ps_next()
                    pst = psums[bt][:, :].bitcast(BF16)

                    def fnp2(e, quarter=quarter, pst=pst):
                        ins = None
                        for q in range(8):
                            r = quarter * 8 + q
                            ins = e.transpose(pst[0:64, q * 128:(q + 1) * 128], vTp[:, r:r + 1009:16], ident_bf[:, :])
                        return ins
                    P.op(PE, fnp2, reads=(B_reg[R_pk], B_const), writes=(B_ps[bt],))
                    P.op(ACT, (lambda e, quarter=quarter, pst=pst: e.activation(
                        out=Vp16[0:64, quarter * 8:(quarter + 1) * 8, :],
                        in_=pst[0:64, 0:1024].rearrange("p (r d) -> p r d", r=8), func=AF.Copy)),
                        reads=(B_ps[bt],), writes=(B_reg[R_pv],))

            def attn_units(specs, kpart, nq):
                pass

            def run_dilation(di, blocks, nq, kparts):
                nb = len(blocks)
                per_bank = 512 // nq
                nbank = (nb + per_bank - 1) // per_bank
                bN = []; bD = []
                for _ in range(nbank):
                    b_ = ps_next(); ps_state["held"].add(b_); bN.append(b_)
                    b_ = ps_next(); ps_state["held"].add(b_); bD.append(b_)
                bi = 0
                unit = 0
                while bi < nb:
                    ub = blocks[bi:bi + (512 // (2 * nq) if True else 2)]
                    bs = ps_next()
                    tiles = []
                    colp = 0
                    PT = PTa if unit % 2 == 0 else PTb
                    for (qc, keys) in ub:
                        for key_ in keys:
                            (kc_, bias_, vt_) = key_[0:3]
                            idn = key_[3] if len(key_) > 3 else ident_bf[0:kparts, 0:kparts]
                            tiles.append((qc, kc_, bias_, vt_, colp, idn))
                            colp += nq

                    def fnS(e, tiles=tiles, bs=bs):
                        ins = None
                        for (qc, kc_, bias_, vt_, cp, idn) in tiles:
                            e.matmul(psums[bs][0:kparts, cp:cp + nq], kc_, qc, start=True, stop=False)
                            ins = e.matmul(psums[bs][0:kparts, cp:cp + nq], idn, bias_,
                                           start=False, stop=True)
                        return ins
                    P.op(PE, fnS, reads=(B_reg[R_qk], B_reg[R_pk], B_const), writes=(B_ps[bs],))
                    P.op(ACT, (lambda e, bs=bs, PT=PT, colp=colp: e.activation(
                        out=PT[0:kparts, 0:colp], in_=psums[bs][0:kparts, 0:colp], func=AF.Exp)),
                        reads=(B_ps[bs],), writes=(B_reg[R_vp],))
                    ti_ = 0

                    def fnV(e, ub=ub, bi=bi, PT=PT, tiles=tiles):
                        ins = None
                        idx = 0
                        for u_, (qc, keys) in enumerate(ub):
                            blk = bi + u_
                            bank = blk // per_bank
                            oc0 = (blk % per_bank) * nq
                            nk = len(keys)
                            for ki in range(nk):
                                (_, _, _, vt_, cp, _i) = tiles[idx + ki]
                                e.matmul(psums[bN[bank]][:, oc0:oc0 + nq], vt_, PT[0:kparts, cp:cp + nq],
                                         start=(ki == 0), stop=(ki == nk - 1))
                            for ki in range(nk):
                                (_, _, _, vt_, cp, _i) = tiles[idx + ki]
                                ins = e.matmul(psums[bD[bank]][:, oc0:oc0 + nq], ones_bf[0:kparts, :],
                                               PT[0:kparts, cp:cp + nq], start=(ki == 0), stop=(ki == nk - 1))
                            idx += nk
                        return ins
                    P.op(PE, fnV, reads=(B_reg[R_vp], B_reg[R_vd], B_reg[R_s], B_reg[R_pv], B_reg[R_pv2], B_const),
                         writes=tuple(B_ps[b_] for b_ in bN + bD))
                    bi += len(ub)
                    unit += 1
                return bN, bD

            def bias_ap(di, o, nk, nq, k0=0, q0=0):
                return biasT[k0:k0 + nk, 2 * di - (h + 1) + 8, o, q0:q0 + nq]

            blocks = []
            for n in range(8):
                qc = qT[:, n * 128:(n + 1) * 128]
                keys = [(kT[:, n * 128:(n + 1) * 128], bias_ap(0, 0, 128, 128), Vd1[:, n, :])]
                if n >= 1:
                    keys.append((kT[:, (n - 1) * 128:n * 128], bias_ap(0, 1, 128, 128), Vd1[:, n - 1, :]))
                elif has_pred:
                    keys.append((kTp[:, 896:1024], bias_ap(0, 1, 128, 128), Vp1))
                blocks.append((qc, keys))
            bN, bD = run_dilation(0, blocks, 128, 128)
            for half in range(2):
                P.op(DVE, (lambda e, half=half, b_=bN[half]: e.tensor_copy(
                    out=Nsb[:, half * 512:(half + 1) * 512], in_=psums[b_][:, :])),
                    reads=(B_ps[bN[half]],), writes=(B_reg[R_N],))
                P.op(DVE, (lambda e, half=half, b_=bD[half]: e.tensor_copy(
                    out=Dsb[:, half * 512:(half + 1) * 512], in_=psums[b_][:, :])),
                    reads=(B_ps[bD[half]],), writes=(B_reg[R_D],))
            for b_ in bN + bD:
                ps_state["held"].discard(b_)
            blocks = []
            for r in range(4):
                for n in range(2):
                    def cols(tn_, n_=None, r=r):
                        return tn_[:, 512 * n_ + r: 512 * n_ + r + 509: 4]
                    qc = cols(qT, n)
                    keys = [(cols(kT, n), bias_ap(1, 0, 128, 128), Vd4[:, r * 2 + n, :])]
                    if n >= 1:
                        keys.append((cols(kT, n - 1), bias_ap(1, 1, 128, 128), Vd4[:, r * 2 + n - 1, :]))
                    elif has_pred:
                        keys.append((cols(kTp, 1), bias_ap(1, 1, 128, 128), Vp4[:, r, :]))
                    blocks.append((qc, keys))
            bN, bD = run_dilation(1, blocks, 128, 128)
            for half in range(2):
                for (acc, bl, rr) in ((Nsb, bN, R_N), (Dsb, bD, R_D)):
                    for r2 in range(2):
                        r = half * 2 + r2
                        dst = acc[:, 0:1024].rearrange("p (n i r) -> p r n i", n=2, r=4)[:, r, :, :]
                        src = psums[bl[half]][:, r2 * 256:(r2 + 1) * 256].rearrange("p (n i) -> p n i", n=2)
                        P.op(DVE, (lambda e, dst=dst, src=src: e.tensor_tensor(out=dst, in0=src, in1=dst, op=ALU.add)),
                             reads=(B_ps[bl[half]], B_reg[rr]), writes=(B_reg[rr],))
            for b_ in bN + bD:
                ps_state["held"].discard(b_)
            blocks = []
            for r in range(16):
                qc = qT[:, r: r + 1009: 16]
                keys = [(kT[:, r: r + 1009: 16], bias_ap(2, 0, 64, 64), Vd16[0:64, r, :])]
                if has_pred:
                    keys.append((kTp[:, r: r + 1009: 16], bias_ap(2, 1, 64, 64, k0=64, q0=0), Vp16[0:64, r, :],
                                 ident_bf[64:128, 64:128]))
                blocks.append((qc, keys))
            bN, bD = run_dilation(2, blocks, 64, 64)
            for half in range(2):
                for (acc, bl, rr) in ((Nsb, bN, R_N), (Dsb, bD, R_D)):
                    dst = acc[:, 0:1024].rearrange("p (m r) -> p r m", r=16)[:, half * 8:(half + 1) * 8, :]
                    src = psums[bl[half]][:, :].rearrange("p (r m) -> p r m", r=8)
                    P.op(DVE, (lambda e, dst=dst, src=src: e.tensor_tensor(out=dst, in0=src, in1=dst, op=ALU.add)),
                         reads=(B_ps[bl[half]], B_reg[rr]), writes=(B_reg[rr],))
            for b_ in bN + bD:
                ps_state["held"].discard(b_)

            R_c = 9
            kc32 = rf(R_c, 0, 384).rearrange("p (g d) -> p g d", g=3)
            vc32 = rf(R_c, 384, 384).rearrange("p (g d) -> p g d", g=3)
            R_c2 = 13
            kcb = rb(R_c2, 0, 384).rearrange("p (g d) -> p g d", g=3)
            vcb = rb(R_c2, 384, 384).rearrange("p (g d) -> p g d", g=3)
            kgT = rb(R_c2, 768, 384).rearrange("p (g d) -> p g d", g=3)
            kbc = rb(R_c2, 1152, 128)
            Ps = rb(R_c2, 1280, 4)
            e0 = rf(R_c, 768, 1)
            tmpc = rf(R_c, 770, 2)
            rows = ((1920, 1), (1536, 4), (0, 16))
            for g, (r0, st_) in enumerate(rows):
                P.dma(SP, (lambda e, g=g, r0=r0, st_=st_: e.dma_start(
                    out=kc32[:, g, :], in_=ck_d[l, r0:r0 + 127 * st_ + 1:st_, h * 128:(h + 1) * 128])),
                    sem_in, writes=(B_reg[R_c],))
                P.dma(SP, (lambda e, g=g, r0=r0, st_=st_: e.dma_start(
                    out=vc32[:, g, :], in_=cv_d[l, r0:r0 + 127 * st_ + 1:st_, h * 128:(h + 1) * 128])),
                    sem_in, writes=(B_reg[R_c],))
            P.op(DVE, (lambda e: e.tensor_copy(out=kcb, in_=kc32)), reads=(B_reg[R_c],), writes=(B_reg[R_c2],))
            P.op(DVE, (lambda e: e.tensor_copy(out=vcb, in_=vc32)), reads=(B_reg[R_c],), writes=(B_reg[R_c2],))
            bt = ps_next()
            pst = psums[bt][:, :].bitcast(BF16)

            def fnT(e, pst=pst):
                ins = None
                for g in range(3):
                    ins = e.transpose(pst[:, g * 128:(g + 1) * 128], kcb[:, g, :], ident_bf[:, :])
                return ins
            P.op(PE, fnT, reads=(B_reg[R_c2], B_const), writes=(B_ps[bt],))
            P.op(DVE, (lambda e, pst=pst: e.tensor_copy(out=kgT, in_=pst[:, 0:384].rearrange("p (g d) -> p g d", g=3))),
                 reads=(B_ps[bt],), writes=(B_reg[R_c2],))
            P.op(DVE, (lambda e: e.tensor_scalar(out=kbc, in0=ones_bf[:, :], scalar1=k32[:, T:TC], scalar2=None,
                                                 op0=ALU.mult)),
                 reads=(B_reg[R_k32], B_const), writes=(B_reg[R_c2],))
            bs = ps_next()

            def fnS2(e, bs=bs):
                ins = None
                for g in range(3):
                    ins = e.matmul(psums[bs][:, g:g + 1], kgT[:, g, :], qT[:, T:TC], start=True, stop=True)
                ins = e.matmul(psums[bs][:, 4:5], kbc, qT[:, T:TC], start=True, stop=True)
                return ins
            P.op(PE, fnS2, reads=(B_reg[R_c2], B_reg[R_qk]), writes=(B_ps[bs],))
            for g in range(3):
                P.op(ACT, (lambda e, g=g, bs=bs: e.activation(
                    out=Ps[:, g:g + 1], in_=psums[bs][:, g:g + 1], func=AF.Exp,
                    bias=small[:, 20 + h * 3 + g:21 + h * 3 + g], scale=1.0)),
                    reads=(B_ps[bs], B_const), writes=(B_reg[R_c2],))
            P.op(ACT, (lambda e, bs=bs: e.activation(out=e0, in_=psums[bs][:, 4:5], func=AF.Exp,
                                                     bias=small[:, 18:19], scale=1.0)),
                 reads=(B_ps[bs], B_const), writes=(B_reg[R_c],))
            bn = ps_next()

            def fnV2(e, bn=bn):
                ins = None
                for g in range(3):
                    e.matmul(psums[bn][:, 0:1], vcb[:, g, :], Ps[:, g:g + 1], start=(g == 0), stop=(g == 2))
                for g in range(3):
                    ins = e.matmul(psums[bn][:, 2:3], ones_bf[:, :], Ps[:, g:g + 1], start=(g == 0), stop=(g == 2))
                return ins
            P.op(PE, fnV2, reads=(B_reg[R_c2], B_const), writes=(B_ps[bn],))
            P.op(DVE, (lambda e, bn=bn: e.scalar_tensor_tensor(
                out=Nsb[:, T:TC], in0=v32[:, T:TC], scalar=e0, in1=psums[bn][:, 0:1], op0=ALU.mult, op1=ALU.add)),
                reads=(B_ps[bn], B_reg[R_c], B_reg[R_v32]), writes=(B_reg[R_N],))
            P.op(DVE, (lambda e, bn=bn: e.tensor_tensor(
                out=Dsb[:, T:TC], in0=psums[bn][:, 2:3], in1=e0, op=ALU.add)),
                reads=(B_ps[bn], B_reg[R_c]), writes=(B_reg[R_D],))
            P.op(DVE, (lambda e: e.reciprocal(out=Dsb[:, 0:TC], in_=Dsb[:, 0:TC])),
                 reads=(B_reg[R_D],), writes=(B_reg[R_D],))
            P.op(DVE, (lambda e: e.tensor_tensor(out=outc[:, 0:TC], in0=Nsb[:, 0:TC], in1=Dsb[:, 0:TC], op=ALU.mult)),
                 reads=(B_reg[R_D], B_reg[R_N]), writes=(B_reg[R_vo],))
            wout_accum(l, sy, [WO], [outc], [R_vo])

        def b_pair(l, cp):
            R_qk, R_qd, R_ks, R_v, R_sg, R_in, R_o0, R_o1, R_mix, R_st, R_bq, R_m = 0, 1, 2, 3, 4, 5, 6, 7, 8, 9, 10, 11
            R_m2 = 12
            sx = slot_load([(lambda t: t[:, 0:2048].rearrange("p (c n) -> p c n", c=16),
                             wcols(w_in_d, l, 3072 + cp * 128, 128)),
                            (lambda t: t[:, 2048:4096].rearrange("p (c n) -> p c n", c=16),
                             wcols(w_in_d, l, 3584 + cp * 128, 128))])
            sy = slot_load([(lambda t: v3(t, 256), wcols(w_in_d, l, 4096 + cp * 256, 256))])
            sz = slot_load([(lambda t: v3(t, 256), wcols(w_in_d, l, 5120 + cp * 256, 256))])
            sw = slot_load([(lambda t: t[:, :].rearrange("p (h n) -> p h n", h=2),
                             w_out_d[l][1024 + cp * 256:1024 + (cp + 1) * 256, :].rearrange("(h p) n -> p h n", p=128))])
            WXq = slots[sx][:, 0:2048].rearrange("p (c n) -> p c n", c=16)
            WXk = slots[sx][:, 2048:4096].rearrange("p (c n) -> p c n", c=16)
            WY = v3(slots[sy], 256); WZ = v3(slots[sz], 256)
            WO = slots[sw][:, :].rearrange("p (h n) -> p h n", h=2)
            qT = rb(R_qk, 0, TP); kT = rb(R_qk, TP, TP)
            qd = rb(R_qd, 0, T)
            kdB = rb(R_ks, 0, 1024).rearrange("p (n d) -> p n d", n=8)
            Sbf = rb(R_ks, 1024, 1024).rearrange("p (n e) -> p n e", n=8)
            vB = rb(R_v, 0, 2048).rearrange("p (n e) -> p n e", n=8)
            sg = rb(R_sg, 0, 2 * TP).rearrange("p (h t) -> p h t", h=2)
            inM = rb(R_in, 0, 2048).rearrange("p (h n i) -> p h n i", h=2, n=8)
            o32 = [rf(R_o0, 0, TP), rf(R_o1, 0, TP)]
            mixc = rb(R_mix, 0, 2 * TP).rearrange("p (h t) -> p h t", h=2)
            S32 = rf(R_st, 0, 128); S0s = rf(R_st, 128, 128); tS = rf(R_st, 256, 128); Sn32 = rf(R_st, 384, 128)
            ksB = rf(R_st, 512, 1)
            Snb = rb(R_st, 1040, 128); vbc = rb(R_st, 1168, 256).rearrange("p (h e) -> p h e", h=2)
            vcol = rf(R_st, 514, 2); Sp32 = rf(R_st, 720, 128)
            ob = rb(R_bq, 0, TP); osq = rb(R_bq, TP, TP)
            mean = rf(R_m, 0, TP); rstd = rf(R_m2, 0, TP)
            qdec = cst["qdec"][:, :].rearrange("p (c i) -> p c i", c=4)
            decT = cst["decayT"][:, :].rearrange("p (h i) -> p h i", h=8)
            for ti, (t0, tn) in enumerate(TTS):
                bq = ps_next()
                mm_group(psums[bq][:, 0:tn], [(WXq[:, kc, :], xbf[:, kc, t0:t0 + tn]) for kc in range(NCH)],
                         reads=(B_slot[sx], B_xbf[ti]), writes=(B_ps[bq],))
                P.op(ACT, (lambda e, bq=bq, t0=t0, tn=tn: e.activation(
                    out=qT[:, t0:t0 + tn], in_=psums[bq][:, 0:tn], func=AF.Copy)),
                    reads=(B_ps[bq],), writes=(B_reg[R_qk],))
                bk = ps_next()
                mm_group(psums[bk][:, 0:tn], [(WXk[:, kc, :], xbf[:, kc, t0:t0 + tn]) for kc in range(NCH)],
                         reads=(B_slot[sx], B_xbf[ti]), writes=(B_ps[bk],))
                P.op(DVE, (lambda e, bk=bk, t0=t0, tn=tn: e.tensor_scalar(
                    out=kT[:, t0:t0 + tn], in0=psums[bk][:, 0:tn], scalar1=0.125, scalar2=None, op0=ALU.mult)),
                    reads=(B_ps[bk],), writes=(B_reg[R_qk],))
                if t0 <= T < t0 + tn:
                    P.op(DVE, (lambda e, bk=bk, t0=t0: e.tensor_scalar(
                        out=ksB, in0=psums[bk][:, T - t0:T - t0 + 1], scalar1=0.125, scalar2=None, op0=ALU.mult)),
                        reads=(B_ps[bk],), writes=(B_reg[R_st],))
                    for hh in range(2):
                        bvv = ps_next()
                        mm_group(psums[bvv][:, 0:tn],
                                 [(WY[:, kc, hh * 128:(hh + 1) * 128], xbf[:, kc, t0:t0 + tn]) for kc in range(NCH)],
                                 reads=(B_slot[sy], B_xbf[ti]), writes=(B_ps[bvv],))
                        P.op(DVE, (lambda e, bvv=bvv, hh=hh, t0=t0: e.tensor_copy(
                            out=vcol[:, hh:hh + 1], in_=psums[bvv][:, T - t0:T - t0 + 1])),
                            reads=(B_ps[bvv],), writes=(B_reg[R_st],))
            P.op(DVE, (lambda e: e.tensor_tensor(
                out=qd[:, 0:T].rearrange("p (n i) -> p n i", n=8),
                in0=qT[:, 0:T].rearrange("p (n i) -> p n i", n=8),
                in1=qdec[:, cp, :].unsqueeze(1).broadcast_to([128, 8, 128]), op=ALU.mult)),
                reads=(B_reg[R_qk], B_const), writes=(B_reg[R_qd],))
            if DBG.get('b_stop', 99) <= 1:
                return
            for hh in range(2):
                for ti, (t0, tn) in enumerate(TTS):
                    bg = ps_next()
                    mm_group(psums[bg][:, 0:tn],
                             [(WZ[:, kc, hh * 128:(hh + 1) * 128], xbf[:, kc, t0:t0 + tn]) for kc in range(NCH)],
                             reads=(B_slot[sz], B_xbf[ti]), writes=(B_ps[bg],))
                    P.op(ACT, (lambda e, bg=bg, hh=hh, t0=t0, tn=tn: e.activation(
                        out=sg[:, hh, t0:t0 + tn], in_=psums[bg][:, 0:tn], func=AF.Silu)),
                        reads=(B_ps[bg],), writes=(B_reg[R_sg],))
            if DBG.get('b_stop', 99) <= 2:
                return
            for n in range(8):
                bt = ps_next()

                def fn(e, n=n, bt=bt):
                    for kc in range(NCH):
                        e.matmul(psums[bt][:, 0:128], xbf[:, kc, n * 128:(n + 1) * 128], WXk[:, kc, :],
                                 start=(kc == 0), stop=(kc == NCH - 1))
                    ins = None
                    for kc in range(NCH):
                        ins = e.matmul(psums[bt][:, 128:384], xbf[:, kc, n * 128:(n + 1) * 128], WY[:, kc, :],
                                       start=(kc == 0), stop=(kc == NCH - 1))
                    return ins
                P.op(PE, fn, reads=(B_slot[sx], B_slot[sy]) + xbf_bufs(n * 128, (n + 1) * 128), writes=(B_ps[bt],))
                for hh in range(2):
                    hcol = 8 + 2 * cp + hh
                    if DBG.get('b3', 3) < 2:
                        continue
                    P.op(DVE, (lambda e, n=n, bt=bt, hh=hh, hcol=hcol: e.tensor_scalar(
                        out=kdB[:, n, hh * 64:(hh + 1) * 64], in0=psums[bt][:, hh * 64:(hh + 1) * 64],
                        scalar1=small[:, hcol:hcol + 1], scalar2=None, op0=ALU.mult)),
                        reads=(B_ps[bt], B_const), writes=(B_reg[R_ks],))
                if DBG.get('b3', 3) < 3:
                    continue
                P.op(DVE, (lambda e, n=n, bt=bt: e.tensor_copy(
                    out=vB[:, n, :], in_=psums[bt][:, 128:384])),
                    reads=(B_ps[bt],), writes=(B_reg[R_v],))
            if DBG.get('b_stop', 99) <= 3:
                return
            bS = [ps_next(), ps_next()]
            for b_ in bS:
                ps_state["held"].add(b_)
            for half in range(2):
                def fn(e, half=half):
                    ins = None
                    for q in range(4):
                        n = half * 4 + q
                        for hh in range(2):
                            ins = e.matmul(psums[bS[half]][hh * 64:(hh + 1) * 64, q * 128:(q + 1) * 128],
                                           kdB[:, n, hh * 64:(hh + 1) * 64], vB[:, n, hh * 128:(hh + 1) * 128],
                                           start=True, stop=True)
                    return ins
                P.op(PE, fn, reads=(B_reg[R_ks], B_reg[R_v]), writes=(B_ps[bS[half]],))
            g128 = small[:, cp:cp + 1]
            has_pred = (cur['pj'] == 1)
            if has_pred:
                P.dma(SP, (lambda e: e.dma_start(out=Sp32, in_=Sps_d[l, cp, :, :])), sem_in,
                      reads=(B_sps[l][cp],), writes=(B_reg[R_st],))
                P.op(DVE, (lambda e: e.tensor_copy(out=Sbf[:, 0, :], in_=Sp32)),
                     reads=(B_reg[R_st],), writes=(B_reg[R_ks],))
            for n in range(8):
                src = psums[bS[n // 4]][:, (n % 4) * 128:(n % 4 + 1) * 128]
                if n == 0 and not has_pred:
                    P.op(DVE, (lambda e, src=src: e.tensor_copy(out=S32, in_=src)),
                         reads=(B_ps[bS[0]],), writes=(B_reg[R_st],))
                else:
                    prev = Sp32 if n == 0 else S32
                    P.op(DVE, (lambda e, src=src, prev=prev: e.scalar_tensor_tensor(
                        out=S32, in0=prev, scalar=g128, in1=src, op0=ALU.mult, op1=ALU.add)),
                        reads=(B_ps[bS[n // 4]], B_reg[R_st], B_const), writes=(B_reg[R_st],))
                if n < 7:
                    P.op(DVE, (lambda e, n=n: e.tensor_copy(out=Sbf[:, n + 1, :], in_=S32)),
                         reads=(B_reg[R_st],), writes=(B_reg[R_ks],))
            for b_ in bS:
                ps_state["held"].discard(b_)
            P.dma(SP, (lambda e, pj=cur['pj']: e.dma_start(out=rp_o[pj, l, cp, :, :], in_=S32)), sem_out, reads=(B_reg[R_st],))
            if cur['pj'] == 0 and n_pass == 2:
                P.dma(SP, (lambda e: e.dma_start(out=Sps_d[l, cp, :, :], in_=S32)), sem_out, reads=(B_reg[R_st],),
                      writes=(B_sps[l][cp],))
            if DBG.get('b_stop', 99) <= 4:
                return
            P.dma(SP, (lambda e: e.dma_start(out=S0s, in_=st_d[l, cp, :, :])), sem_in, writes=(B_reg[R_st],))
            for hh in range(2):
                P.op(DVE, (lambda e, hh=hh: e.tensor_scalar(out=vbc[:, hh, :], in0=ones_bf[:, :], scalar1=vcol[:, hh:hh + 1],
                                                            scalar2=None, op0=ALU.mult)),
                     reads=(B_reg[R_st], B_const), writes=(B_reg[R_st],))
                bt = ps_next()
                P.op(PE, (lambda e, bt=bt, hh=hh: e.matmul(psums[bt][:, 0:128], vbc[:, hh, :], ident_bf[:, :],
                                                          start=True, stop=True)),
                     reads=(B_reg[R_st], B_const), writes=(B_ps[bt],))
                P.op(DVE, (lambda e, hh=hh, bt=bt: e.tensor_scalar(
                    out=tS[hh * 64:(hh + 1) * 64, :], in0=psums[bt][hh * 64:(hh + 1) * 64, 0:128],
                    scalar1=ksB[hh * 64:(hh + 1) * 64, :], scalar2=None, op0=ALU.mult)),
                    reads=(B_ps[bt], B_reg[R_st]), writes=(B_reg[R_st],))
            P.op(DVE, (lambda e: e.scalar_tensor_tensor(out=Sn32, in0=S0s, scalar=small[:, 4 + cp:5 + cp], in1=tS,
                                                        op0=ALU.mult, op1=ALU.add)),
                 reads=(B_reg[R_st], B_const), writes=(B_reg[R_st],))
            P.op(DVE, (lambda e: e.tensor_copy(out=Snb, in_=Sn32)), reads=(B_reg[R_st],), writes=(B_reg[R_st],))
            P.dma(SP, (lambda e: e.dma_start(out=rs_o[l, cp, :, :], in_=Sn32)), sem_out, reads=(B_reg[R_st],))
            if DBG.get('b_stop', 99) <= 5:
                return
            for hh in range(2):
                h = 2 * cp + hh
                po = hh * 64
                for half in range(2):
                    bi_ = ps_next()

                    def fn(e, half=half, bi_=bi_, po=po):
                        ins = None
                        for q in range(4):
                            n = half * 4 + q
                            ins = e.matmul(psums[bi_][:, q * 128:(q + 1) * 128], kT[po:po + 64, n * 128:(n + 1) * 128],
                                           qT[po:po + 64, n * 128:(n + 1) * 128], start=True, stop=True)
                        return ins
                    P.op(PE, fn, reads=(B_reg[R_qk],), writes=(B_ps[bi_],))
                    P.op(DVE, (lambda e, half=half, bi_=bi_, hh=hh, h=h: e.tensor_tensor(
                        out=inM[:, hh, half * 4:(half + 1) * 4, :],
                        in0=psums[bi_][:, :].rearrange("p (n i) -> p n i", n=4),
                        in1=decT[:, h, :].unsqueeze(1).broadcast_to([128, 4, 128]), op=ALU.mult)),
                        reads=(B_ps[bi_], B_const), writes=(B_reg[R_in],))
                for half in range(2):
                    bo = ps_next()

                    def fn(e, half=half, bo=bo, hh=hh, po=po):
                        ins = None
                        for q in range(4):
                            n = half * 4 + q
                            ins = e.matmul(psums[bo][:, q * 128:(q + 1) * 128], vB[:, n, hh * 128:(hh + 1) * 128],
                                           inM[:, hh, n, :], start=True, stop=(n == 0 and not has_pred))
                            if n >= 1 or has_pred:
                                ins = e.matmul(psums[bo][:, q * 128:(q + 1) * 128], Sbf[po:po + 64, n, :],
                                               qd[po:po + 64, n * 128:(n + 1) * 128], start=False, stop=True)
                        return ins
                    P.op(PE, fn, reads=(B_reg[R_v], B_reg[R_in], B_reg[R_ks], B_reg[R_qd]), writes=(B_ps[bo],))
                    P.op(ACT, (lambda e, half=half, bo=bo, hh=hh: e.activation(
                        out=o32[hh][:, half * 512:(half + 1) * 512], in_=psums[bo][:, :], func=AF.Copy)),
                        reads=(B_ps[bo],), writes=(B_reg[(R_o0, R_o1)[hh]],))
                bo = ps_next()
                P.op(PE, (lambda e, bo=bo, po=po: e.matmul(psums[bo][:, 0:1], Snb[po:po + 64, :], qT[po:po + 64, T:TC],
                                                          start=True, stop=True)),
                     reads=(B_reg[R_st], B_reg[R_qk]), writes=(B_ps[bo],))
                P.op(ACT, (lambda e, bo=bo, hh=hh: e.activation(out=o32[hh][:, T:TC], in_=psums[bo][:, 0:1], func=AF.Copy)),
                     reads=(B_ps[bo],), writes=(B_reg[(R_o0, R_o1)[hh]],))
                ro = (R_o0, R_o1)[hh]
                gcol = l * 8 + h
                P.op(ACT, (lambda e, hh=hh: e.activation(out=ob[:, 0:TC], in_=o32[hh][:, 0:TC], func=AF.Copy)),
                     reads=(B_reg[ro],), writes=(B_reg[R_bq],))
                P.op(ACT, (lambda e, hh=hh: e.activation(out=osq[:, 0:TC], in_=o32[hh][:, 0:TC], func=AF.Square)),
                     reads=(B_reg[ro],), writes=(B_reg[R_bq],))
                for ti, (t0, tn) in enumerate(TTS):
                    b1 = ps_next(); b2 = ps_next()
                    P.op(PE, (lambda e, b1=b1, t0=t0, tn=tn: e.matmul(psums[b1][:, 0:tn], ones_bf[:, :], ob[:, t0:t0 + tn],
                                                                      start=True, stop=True)),
                         reads=(B_reg[R_bq], B_const), writes=(B_ps[b1],))
                    P.op(PE, (lambda e, b2=b2, t0=t0, tn=tn: e.matmul(psums[b2][:, 0:tn], ones_bf[:, :], osq[:, t0:t0 + tn],
                                                                      start=True, stop=True)),
                         reads=(B_reg[R_bq], B_const), writes=(B_ps[b2],))
                    P.op(DVE, (lambda e, b1=b1, t0=t0, tn=tn: e.tensor_scalar(
                        out=mean[:, t0:t0 + tn], in0=psums[b1][:, 0:tn], scalar1=1.0 / 128, scalar2=None, op0=ALU.mult)),
                        reads=(B_ps[b1],), writes=(B_reg[R_m],))
                    P.op(DVE, (lambda e, t0=t0, tn=tn: e.tensor_tensor(
                        out=rstd[:, t0:t0 + tn], in0=mean[:, t0:t0 + tn], in1=mean[:, t0:t0 + tn], op=ALU.mult)),
                        reads=(B_reg[R_m],), writes=(B_reg[R_m2],))
                    P.op(DVE, (lambda e, b2=b2, t0=t0, tn=tn: e.scalar_tensor_tensor(
                        out=rstd[:, t0:t0 + tn], in0=psums[b2][:, 0:tn], scalar=1.0 / 128, in1=rstd[:, t0:t0 + tn],
                        op0=ALU.mult, op1=ALU.subtract)),
                        reads=(B_ps[b2], B_reg[R_m2]), writes=(B_reg[R_m2],))
                P.op(ACT, (lambda e: e.activation(out=rstd[:, 0:TC], in_=rstd[:, 0:TC], func=AF.Sqrt,
                                                  bias=small[:, 17:18], scale=1.0)),
                     reads=(B_reg[R_m2], B_const), writes=(B_reg[R_m2],))
                P.op(DVE, (lambda e: e.reciprocal(out=rstd[:, 0:TC], in_=rstd[:, 0:TC])),
                     reads=(B_reg[R_m2],), writes=(B_reg[R_m2],))
                P.op(DVE, (lambda e, hh=hh: e.tensor_tensor(out=o32[hh][:, 0:TC], in0=o32[hh][:, 0:TC], in1=mean[:, 0:TC],
                                                            op=ALU.subtract)),
                     reads=(B_reg[ro], B_reg[R_m]), writes=(B_reg[ro],))
                P.op(DVE, (lambda e, hh=hh: e.tensor_tensor(out=o32[hh][:, 0:TC], in0=o32[hh][:, 0:TC], in1=rstd[:, 0:TC],
                                                            op=ALU.mult)),
                     reads=(B_reg[ro], B_reg[R_m2]), writes=(B_reg[ro],))
                P.op(ACT, (lambda e, hh=hh, gcol=gcol: e.activation(
                    out=o32[hh][:, 0:TC], in_=o32[hh][:, 0:TC], func=AF.Identity,
                    bias=gnb[:, gcol:gcol + 1], scale=gng[:, gcol:gcol + 1])),
                    reads=(B_reg[ro], B_const), writes=(B_reg[ro],))
                P.op(DVE, (lambda e, hh=hh: e.tensor_tensor(out=mixc[:, hh, 0:TC], in0=o32[hh][:, 0:TC],
                                                            in1=sg[:, hh, 0:TC], op=ALU.mult)),
                     reads=(B_reg[ro], B_reg[R_sg]), writes=(B_reg[R_mix],))
            if DBG.get('b_stop', 99) <= 6:
                return
            wout_accum(l, sw, [WO[:, 0, :], WO[:, 1, :]], [mixc[:, 0, :], mixc[:, 1, :]], [R_mix])

        for pj in range(n_pass):
            cur['pj'] = pj
            load_x(pj)
            for l in range(n_layers):
                if do_ffn:
                    ffn(l, 0)
                layer_norm(l * 4 + 0, final=(stop_after == 'a'))
                if stop_after == 'a':
                    break
                if do_mix:
                    for cp in range(4):
                        if DBG.get('b', 1):
                            b_pair(l, cp)
                        if DBG.get('a', 1):
                            a_head(l, 2 * cp)
                            a_head(l, 2 * cp + 1)
                layer_norm(l * 4 + 1, final=(stop_after == 'b'))
                if stop_after == 'b':
                    break
                if do_ffn:
                    ffn(l, 1)
                layer_norm(l * 4 + 2, final=(stop_after == 'c'))
                if stop_after == 'c':
                    break
                if do_ple:
                    ple(l)
                layer_norm(l * 4 + 3, final=(l == n_layers - 1))
            yT_v = yT_o[pj].rearrange("(c p) t -> p c t", p=128)
            for c in range(NCH):
                P.dma(SP, (lambda e, c=c, yT_v=yT_v: e.dma_start(out=yT_v[:, c, :], in_=x32[:, c, 0:TC])), sem_out,
                      reads=tuple(B_x32[c]))
        P.dma(SP, (lambda e: e.dma_start(out=ks_o[:, :], in_=ks32[:, :])), sem_out, reads=(B_ks,))
        P.dma(SP, (lambda e: e.dma_start(out=vs_o[:, :], in_=vs32[:, :])), sem_out, reads=(B_ks,))
        final_waits = [(d.h, d.count) for d in sp_pool if d.count > 0]

        with nc.Block() as block:
            @block.tensor
            def _(e):
                P.replay(PE, e)

            @block.scalar
            def _(e):
                P.replay(ACT, e)

            @block.vector
            def _(e):
                P.replay(DVE, e)

            @block.gpsimd
            def _(e):
                P.replay(POOL, e)

            @block.sync
            def _(e):
                P.replay(SP, e)
                for s, v in final_waits:
                    e.wait_ge(s, v)
    return nc


_CACHE = {}


def _get_nc():
    if "nc" not in _CACHE:
        _CACHE["nc"] = build()
    return _CACHE["nc"]


def _fm_cols(a):
    a = np.asarray(a, np.float32)
    F = a.shape[-1]
    return np.ascontiguousarray(a.reshape(-1, F // 128, 128).transpose(2, 0, 1).reshape(128, -1))


def kernel(x_prompt, x_sample, cache_k, cache_v, state_ret, p_prompt, p_sample,
           w_in, w_out, gn_g, gn_b, ffn1_w1, ffn1_w3, ffn1_w2, ffn2_w1, ffn2_w3, ffn2_w2,
           w_ple, w_gate, ln_g, ln_b):
    nc = _get_nc()
    consts = _consts()
    f32 = lambda a: np.ascontiguousarray(np.asarray(a, np.float32))
    shared = {
        "w_in": f32(w_in), "w_out": f32(w_out), "ffn1_w1": f32(ffn1_w1), "ffn1_w3": f32(ffn1_w3),
        "ffn1_w2": f32(ffn1_w2), "ffn2_w1": f32(ffn2_w1), "ffn2_w3": f32(ffn2_w3), "ffn2_w2": f32(ffn2_w2),
        "w_ple": f32(w_ple), "w_gate": f32(w_gate),
        "lng": _fm_cols(ln_g), "lnb": _fm_cols(ln_b), "gng": _fm_cols(gn_g), "gnb": _fm_cols(gn_b),
    }
    shared.update(consts)
    x_prompt = np.asarray(x_prompt, np.float32); x_sample = np.asarray(x_sample, np.float32)
    p_prompt = np.asarray(p_prompt, np.float32); p_sample = np.asarray(p_sample, np.float32)
    cache_k = np.asarray(cache_k, np.float32); cache_v = np.asarray(cache_v, np.float32)
    state_ret = np.asarray(state_ret, np.float32)
    in_maps = []
    for c in range(8):
        s = c // 2
        xT = np.stack([np.concatenate([x_prompt[s, j * T:(j + 1) * T, :].T, x_sample[c, 0, :, None]], axis=1)
                       for j in range(2)])
        pT = np.stack([np.concatenate([p_prompt[:, s, j * T:(j + 1) * T, :].transpose(0, 2, 1),
                                       p_sample[:, c, 0, :, None]], axis=2) for j in range(2)])
        m = dict(shared)
        m["xT"] = np.ascontiguousarray(xT)
        m["pT"] = np.ascontiguousarray(pT)
        m["ck"] = np.ascontiguousarray(cache_k[:, c].reshape(DEPTH, 2048, 1024))
        m["cv"] = np.ascontiguousarray(cache_v[:, c].reshape(DEPTH, 2048, 1024))
        m["st"] = np.ascontiguousarray(state_ret[:, c].reshape(DEPTH, 4, 128, 128))
        in_maps.append(m)
    res = run_bass_kernel_spmd(nc, in_maps, core_ids=list(range(8)))
    R = res.results
    y_prompt = np.zeros((4, 2048, D), np.float32)
    y_sample = np.zeros((8, 1, D), np.float32)
    k_prompt = np.zeros((DEPTH, 4, 2048, 8, 128), np.float32)
    v_prompt = np.zeros((DEPTH, 4, 2048, 8, 128), np.float32)
    ret_prompt = np.zeros((DEPTH, 4, 8, 64, 128), np.float32)
    k_sample = np.zeros((DEPTH, 8, 1, 8, 128), np.float32)
    v_sample = np.zeros((DEPTH, 8, 1, 8, 128), np.float32)
    ret_sample = np.zeros((DEPTH, 8, 8, 64, 128), np.float32)
    for c in range(8):
        s = c // 2
        r = R[c]
        yT = np.asarray(r["yT"])
        y_sample[c, 0, :] = yT[1, :, T]
        k_sample[:, c, 0] = np.asarray(r["kso"]).reshape(128, DEPTH, 8).transpose(1, 2, 0)
        v_sample[:, c, 0] = np.asarray(r["vso"]).reshape(128, DEPTH, 8).transpose(1, 2, 0)
        ret_sample[:, c] = np.asarray(r["rso"]).reshape(DEPTH, 8, 64, 128)
        if c % 2 == 0:
            kTo = np.asarray(r["kTo"]); vTo = np.asarray(r["vTo"])
            for j in range(2):
                y_prompt[s, j * T:(j + 1) * T, :] = yT[j, :, :T].T
                k_prompt[:, s, j * T:(j + 1) * T] = kTo[j].transpose(0, 3, 1, 2)
                v_prompt[:, s, j * T:(j + 1) * T] = vTo[j].transpose(0, 3, 1, 2)
            ret_prompt[:, s] = np.asarray(r["rpo"])[1].reshape(DEPTH, 8, 64, 128)
    return (y_prompt, y_sample, k_prompt, v_prompt, ret_prompt, k_sample, v_sample, ret_sample)
```
